# Optimizing a Trainium2 kernel written in Bass

```python
import jax, jax.numpy as jnp
from jax import lax
import numpy as np

D_MODEL = 2048
BATCH = 2
SEQ = 16384
DEPTH = 4

GRID_W = 64
CTX_LEN = 256
A_HEADS = 8
A_DV = D_MODEL // A_HEADS
A_DK = A_DV // 2
A_WIDTH = A_HEADS * A_DV
A_CHUNK = 64
A_FBIAS_LO = 3.0
A_FBIAS_HI = 6.0
B_HEADS = 16
B_DH = D_MODEL // B_HEADS
B_WIDTH = B_HEADS * B_DH
NA_KH = 8
NA_KW = 16
NA_QB = 16
ROPE_BASE = 10000.0
EPS = 1e-6
IN_NAMES = ('aq', 'ak', 'av', 'ao', 'az', 'ai_f', 'af_f', 'ai_b', 'af_b', 'bq', 'bk', 'bv', 'bz', 'ga', 'gb')
IN_SIZES = (A_HEADS * A_DK, A_HEADS * A_DK, A_WIDTH, A_WIDTH, A_WIDTH, A_HEADS, A_HEADS, A_HEADS, A_HEADS, B_WIDTH, B_WIDTH, B_WIDTH, B_WIDTH, D_MODEL, D_MODEL)
N_IN = 2 * A_HEADS * A_DK + 3 * A_WIDTH + 4 * A_HEADS + 4 * B_WIDTH + 2 * D_MODEL

kernel_name = 'hybrid_mlstm_natten_block'


def _in_offset(name):
    return int(sum(IN_SIZES[:IN_NAMES.index(name)]))


def _split_in(p):
    out = {}
    off = 0
    for name, size in zip(IN_NAMES, IN_SIZES):
        out[name] = p[..., off:off + size]
        off += size
    return out


def _heads(a, h):
    return a.reshape(a.shape[:-1] + (h, a.shape[-1] // h))


def _rms(x, g):
    xf = x.astype(jnp.float32)
    y = xf * lax.rsqrt(jnp.mean(xf * xf, axis=-1, keepdims=True) + EPS)
    return (y * g.astype(jnp.float32)).astype(x.dtype)


def _modulate(x, g, shift, scale):
    return _rms(x, g) * (1 + scale) + shift


def _rope_tables(n_tok):
    t = jnp.arange(n_tok)
    row = (t // GRID_W).astype(jnp.float32)
    col = (t % GRID_W).astype(jnp.float32)
    nf = A_DK // 4
    inv = ROPE_BASE ** (-jnp.arange(nf, dtype=jnp.float32) / nf)
    ar = row[:, None, None] * inv
    ac = col[:, None, None] * inv
    return (jnp.cos(ar), jnp.sin(ar), jnp.cos(ac), jnp.sin(ac))


def _rope(x, tabs):
    cos_r, sin_r, cos_c, sin_c = tabs
    xr, xc = jnp.split(x, 2, axis=-1)

    def rot(u, cs, sn):
        u1, u2 = jnp.split(u, 2, axis=-1)
        return jnp.concatenate([u1 * cs - u2 * sn, u1 * sn + u2 * cs], axis=-1)

    return jnp.concatenate([rot(xr, cos_r, sin_r), rot(xc, cos_c, sin_c)], axis=-1)


def _mlstm_scan(q, k, v, i_pre, f_pre, state, with_out):
    bsz, n_tok, n_heads, _ = k.shape
    nc = n_tok // A_CHUNK

    def chunks(a):
        return jnp.moveaxis(a.reshape((bsz, nc, A_CHUNK) + a.shape[2:]), 1, 0)

    tril = jnp.tril(jnp.ones((A_CHUNK, A_CHUNK), dtype=bool))

    def step(carry, xs):
        C, n, m = carry
        qc, kc, vc, ic, logf = xs
        b = jnp.cumsum(logf, axis=1)
        g = b[:, -1]
        a = g[:, None] - b + ic
        m_new = jnp.maximum(g + m, jnp.max(a, axis=1))
        w_prev = jnp.exp(g + m - m_new)
        w_tok = jnp.exp(a - m_new[:, None])
        C_new = w_prev[..., None, None] * C + jnp.einsum('bsh,bshv,bshd->bhvd', w_tok, vc, kc)
        n_new = w_prev[..., None] * n + jnp.einsum('bsh,bshd->bhd', w_tok, kc)
        h = None
        if with_out:
            log_d = b[:, :, None] - b[:, None] + ic[:, None]
            log_d = jnp.where(tril[None, :, :, None], log_d, -jnp.inf)
            inter = b + m[:, None]
            m_q = jnp.maximum(inter, jnp.max(log_d, axis=2))
            s = jnp.einsum('bjhd,bshd->bjsh', qc, kc) * jnp.exp(log_d - m_q[:, :, None])
            w_int = jnp.exp(inter - m_q)
            num = jnp.einsum('bjsh,bshv->bjhv', s, vc) + w_int[..., None] * jnp.einsum('bhvd,bjhd->bjhv', C, qc)
            den = jnp.sum(s, axis=2) + w_int * jnp.einsum('bhd,bjhd->bjh', n, qc)
            h = num / jnp.maximum(jnp.abs(den), jnp.exp(-m_q))[..., None]
        return (C_new, n_new, m_new), h

    xs = (chunks(q) if with_out else None, chunks(k), chunks(v), chunks(i_pre), chunks(jax.nn.log_sigmoid(f_pre)))
    state, h = lax.scan(step, state, xs)
    if with_out:
        h = jnp.moveaxis(h, 0, 1).reshape(bsz, n_tok, n_heads, -1)
    return h, state


def _rev(a):
    return jnp.flip(a, axis=1)


def _mlstm_out(h, p, g):
    hn = h * lax.rsqrt(jnp.mean(h * h, axis=-1, keepdims=True) + EPS) * g.astype(jnp.float32).reshape(A_HEADS, A_DV)
    y = hn.reshape(h.shape[:2] + (A_WIDTH,)).astype(p['ao'].dtype)
    return y * jax.nn.sigmoid(p['ao']) * jax.nn.silu(p['az'])


def _mlstm_branch(pl, pc, rope_tabs, norm_g, need_ctx):
    f32 = jnp.float32

    def qkv(p, use_rope):
        q = _heads(p['aq'], A_HEADS).astype(f32)
        k = _heads(p['ak'], A_HEADS).astype(f32) * (A_DK ** -0.5)
        v = _heads(p['av'], A_HEADS).astype(f32)
        if use_rope:
            q = _rope(q, rope_tabs)
            k = _rope(k, rope_tabs)
        return q, k, v

    def gates(p):
        return tuple(p[n].astype(f32) for n in ('ai_f', 'af_f', 'ai_b', 'af_b'))

    ql, kl, vl = qkv(pl, True)
    qc, kc, vc = qkv(pc, False)
    il_f, fl_f, il_b, fl_b = gates(pl)
    ic_f, fc_f, ic_b, fc_b = gates(pc)
    bsz = kc.shape[0]
    zero = (jnp.zeros((bsz, A_HEADS, A_DV, A_DK), f32), jnp.zeros((bsz, A_HEADS, A_DK), f32), jnp.zeros((bsz, A_HEADS), f32))
    hc_f, st_f = _mlstm_scan(qc if need_ctx else None, kc, vc, ic_f, fc_f, zero, need_ctx)
    hc_b, st_b = _mlstm_scan(_rev(qc) if need_ctx else None, _rev(kc), _rev(vc), _rev(ic_b), _rev(fc_b), zero, need_ctx)
    hl_f, _ = _mlstm_scan(ql, kl, vl, il_f, fl_f, st_f, True)
    hl_b, _ = _mlstm_scan(_rev(ql), _rev(kl), _rev(vl), _rev(il_b), _rev(fl_b), st_b, True)
    y_lat = _mlstm_out(hl_f + _rev(hl_b), pl, norm_g)
    y_ctx = _mlstm_out(hc_f + _rev(hc_b), pc, norm_g) if need_ctx else None
    return y_lat, y_ctx


def _na_col_tables():
    nblk = GRID_W // NA_QB
    kcw = NA_QB + NA_KW
    j = np.arange(nblk)
    cs = np.clip(j * NA_QB - NA_KW // 2, 0, GRID_W - kcw)
    col_idx = cs[:, None] + np.arange(kcw)[None]
    qcol = j[:, None] * NA_QB + np.arange(NA_QB)[None]
    ws = np.clip(qcol - NA_KW // 2, 0, GRID_W - NA_KW)
    kcol = col_idx[:, None, :]
    mask = (kcol >= ws[..., None]) & (kcol < ws[..., None] + NA_KW)
    dc = np.clip(kcol - qcol[..., None] + NA_KW - 1, 0, 2 * NA_KW - 2)
    return col_idx, mask, dc


def _na_branch(pl, pc, rows, q_g, k_g, rpb, need_ctx):
    f32 = jnp.float32
    scale = B_DH ** -0.5
    ql = _rms(_heads(pl['bq'], B_HEADS), q_g)
    kl = _rms(_heads(pl['bk'], B_HEADS), k_g)
    vl = _heads(pl['bv'], B_HEADS)
    kc = _rms(_heads(pc['bk'], B_HEADS), k_g)
    vc = _heads(pc['bv'], B_HEADS)
    bsz, n_tok = ql.shape[:2]
    kh = min(NA_KH, rows)
    col_idx, col_mask, dc_idx = _na_col_tables()
    nblk, kcw = col_idx.shape
    n_lat = kh * kcw
    mask = np.broadcast_to(col_mask[:, :, None, :], (nblk, NA_QB, kh, kcw)).reshape(nblk, NA_QB, n_lat)
    kg = kl.reshape(bsz, rows, GRID_W, B_HEADS, B_DH)
    vg = vl.reshape(bsz, rows, GRID_W, B_HEADS, B_DH)
    qg = jnp.moveaxis(ql.reshape(bsz, rows, nblk, NA_QB, B_HEADS, B_DH), 1, 0)

    def gather(a, rs):
        blk = lax.dynamic_slice_in_dim(a, rs, kh, axis=1)[:, :, col_idx]
        return blk.transpose(0, 2, 1, 3, 4, 5).reshape(bsz, nblk, n_lat, B_HEADS, B_DH)

    def row_step(args):
        r, q_r = args
        rs = jnp.clip(r - kh // 2, 0, rows - kh)
        k_b = gather(kg, rs)
        v_b = gather(vg, rs)
        dr = rs + jnp.arange(kh) - r + (NA_KH - 1)
        bias = rpb[:, dr][:, :, dc_idx].transpose(0, 2, 3, 1, 4).reshape(B_HEADS, nblk, NA_QB, n_lat)
        s_lat = jnp.einsum('bjuhd,bjkhd->bhjuk', q_r, k_b).astype(f32) * scale + bias.astype(f32)
        s_lat = jnp.where(mask, s_lat, -jnp.inf)
        s_ctx = jnp.einsum('bjuhd,bchd->bhjuc', q_r, kc).astype(f32) * scale
        p = jax.nn.softmax(jnp.concatenate([s_lat, s_ctx], axis=-1), axis=-1).astype(v_b.dtype)
        return (jnp.einsum('bhjuk,bjkhd->bjuhd', p[..., :n_lat], v_b)
                + jnp.einsum('bhjuc,bchd->bjuhd', p[..., n_lat:], vc))

    o = lax.map(row_step, (jnp.arange(rows), qg))
    o = jnp.moveaxis(o, 0, 1).reshape(bsz, n_tok, B_WIDTH)
    y_lat = o * jax.nn.silu(pl['bz'])
    y_ctx = None
    if need_ctx:
        qc = _rms(_heads(pc['bq'], B_HEADS), q_g)
        s = jnp.einsum('bqhd,bkhd->bhqk', qc, kc).astype(f32) * scale
        p = jax.nn.softmax(s, axis=-1).astype(vc.dtype)
        oc = jnp.einsum('bhqk,bkhd->bqhd', p, vc).reshape(bsz, -1, B_WIDTH)
        y_ctx = oc * jax.nn.silu(pc['bz'])
    return y_lat, y_ctx


def _merge(y_a, y_b, p, w_a, w_b, w_o, b_o):
    y = jax.nn.sigmoid(p['ga']) * (y_a @ w_a) + jax.nn.sigmoid(p['gb']) * (y_b @ w_b)
    return y @ w_o + b_o


def setup_inputs(seed: int = 0) -> dict:
    key = jax.random.key(seed)
    ks = jax.random.split(key, 17)

    def nrm(k, shape, s):
        return jax.random.normal(k, shape, jnp.float32) * s

    fbias = jnp.linspace(A_FBIAS_LO, A_FBIAS_HI, A_HEADS, dtype=jnp.float32)
    off_f = _in_offset('af_f')
    off_b = _in_offset('af_b')
    b_in = nrm(ks[8], (DEPTH, N_IN), 0.01)
    b_in = b_in.at[:, off_f:off_f + A_HEADS].add(fbias).at[:, off_b:off_b + A_HEADS].add(fbias)
    return {
        'x': nrm(ks[0], (BATCH, SEQ, D_MODEL), 1.0),
        'c': nrm(ks[1], (BATCH, D_MODEL), 1.0),
        'ctx': nrm(ks[2], (BATCH, CTX_LEN, D_MODEL), 1.0),
        'c_ctx': nrm(ks[3], (D_MODEL,), 1.0),
        'w_mod': nrm(ks[4], (DEPTH, D_MODEL, 3 * D_MODEL), 0.5 * D_MODEL ** -0.5),
        'b_mod': nrm(ks[5], (DEPTH, 3 * D_MODEL), 0.01),
        'norm_g': 1.0 + nrm(ks[6], (DEPTH, D_MODEL), 0.02),
        'w_in': nrm(ks[7], (DEPTH, D_MODEL, N_IN), D_MODEL ** -0.5),
        'b_in': b_in,
        'a_norm_g': 1.0 + nrm(ks[9], (DEPTH, A_WIDTH), 0.02),
        'w_br_a': nrm(ks[10], (DEPTH, A_WIDTH, D_MODEL), A_WIDTH ** -0.5),
        'w_br_b': nrm(ks[11], (DEPTH, B_WIDTH, D_MODEL), B_WIDTH ** -0.5),
        'na_q_g': 1.0 + nrm(ks[12], (DEPTH, B_DH), 0.02),
        'na_k_g': 1.0 + nrm(ks[13], (DEPTH, B_DH), 0.02),
        'na_rpb': nrm(ks[14], (DEPTH, B_HEADS, 2 * NA_KH - 1, 2 * NA_KW - 1), 0.05),
        'w_out': nrm(ks[15], (DEPTH, D_MODEL, D_MODEL), D_MODEL ** -0.5),
        'b_out': nrm(ks[16], (DEPTH, D_MODEL), 0.01),
    }


def reference(x, c, ctx, c_ctx, w_mod, b_mod, norm_g, w_in, b_in, a_norm_g, w_br_a, w_br_b, na_q_g, na_k_g, na_rpb, w_out, b_out):
    n_tok = x.shape[1]
    rows = n_tok // GRID_W
    rope_tabs = _rope_tables(n_tok)
    sc = jax.nn.silu(c)
    sc_ctx = jax.nn.silu(c_ctx)
    for l in range(DEPTH):
        need_ctx = l < DEPTH - 1
        shift, scale, gate = jnp.split(sc @ w_mod[l] + b_mod[l], 3, axis=-1)
        shift_c, scale_c, gate_c = jnp.split(sc_ctx @ w_mod[l] + b_mod[l], 3, axis=-1)
        h_lat = _modulate(x, norm_g[l], shift[:, None], scale[:, None])
        h_ctx = _modulate(ctx, norm_g[l], shift_c, scale_c)
        pl = _split_in(h_lat @ w_in[l] + b_in[l])
        pc = _split_in(h_ctx @ w_in[l] + b_in[l])
        ya_l, ya_c = _mlstm_branch(pl, pc, rope_tabs, a_norm_g[l], need_ctx)
        yb_l, yb_c = _na_branch(pl, pc, rows, na_q_g[l], na_k_g[l], na_rpb[l], need_ctx)
        x = x + gate[:, None] * _merge(ya_l, yb_l, pl, w_br_a[l], w_br_b[l], w_out[l], b_out[l])
        if need_ctx:
            ctx = ctx + gate_c * _merge(ya_c, yb_c, pc, w_br_a[l], w_br_b[l], w_out[l], b_out[l])
    return x
```

```python
import contextlib
import numpy as np
import concourse.bass as bass
import concourse.mybir as mybir
from concourse.bass_utils import run_bass_kernel_spmd

F32 = mybir.dt.float32
BF16 = mybir.dt.bfloat16
AF = mybir.ActivationFunctionType
ALU = mybir.AluOpType
AX = mybir.AxisListType

D = 2048
NIN = 20512
CTX = 256
GRID_W = 64
EPS = 1e-6
OFF = dict(aq=0, ak=1024, av=2048, ao=4096, az=6144, g=8192, bq=8224, bk=10272, bv=12320,
           bz=14368, ga=16416, gb=18464)
NEG = -30000.0


class Buf:
    __slots__ = ("name", "w", "r")

    def __init__(self, name=""):
        self.name = name
        self.w = {}
        self.r = {}


class Sched:
    ENGS = ("pe", "dve", "act", "pool", "sp")
    NCH = 40
    EPOCH = 12000

    def __init__(self, nc, stack):
        self.nc = nc
        self.stack = stack
        self.prog = {e: [] for e in self.ENGS}
        self.cnt = {e: 0 for e in self.ENGS}
        self.sems = {e: [stack.enter_context(nc.semaphore(f"s_{e}_0"))] for e in self.ENGS}
        self.ch_sem = [stack.enter_context(nc.semaphore(f"ch{i}")) for i in range(self.NCH)]
        self.ch_cnt = [0] * self.NCH
        self.ch_next = 0
        self.seen = {e: {} for e in self.ENGS}
        self.last = {e: None for e in self.ENGS}
        self.n_instr = 0

    def _cur(self, e):
        if self.cnt[e] >= self.EPOCH:
            self.sems[e].append(self.stack.enter_context(self.nc.semaphore(f"s_{e}_{len(self.sems[e])}")))
            self.cnt[e] = 0
        return len(self.sems[e]) - 1

    def _waits(self, e, reads, writes, pwrites):
        need = {}

        def add(d):
            for key, val in d.items():
                if need.get(key, 0) < val:
                    need[key] = val
        for b in reads:
            add(b.w)
        for b in writes:
            add(b.w)
            add(b.r)
        for b in pwrites:
            add(b.r)
        out = []
        seen = self.seen[e]
        for key, val in need.items():
            if seen.get(key, 0) >= val:
                continue
            seen[key] = val
            out.append((key, val))
        return out

    def _sem(self, key):
        if key[0] == "e":
            return self.sems[key[1]][key[2]]
        return self.ch_sem[key[1]]

    def _commit(self, tok, reads, writes, pwrites):
        key, val = tok
        for b in reads:
            b.r[key] = val
        for b in writes:
            b.w = {key: val}
            b.r = {}
        for b in pwrites:
            b.w[key] = val

    def op(self, e, meth, kw, reads=(), writes=(), pwrites=()):
        ep = self._cur(e)
        waits = self._waits(e, reads, writes, pwrites)
        self.cnt[e] += 1
        tok = (("e", e, ep), self.cnt[e])
        self.last[e] = tok
        self.prog[e].append(("op", meth, kw, waits, tok))
        self._commit(tok, reads, writes, pwrites)
        self.n_instr += 1

    def dma(self, q, kw, reads=(), writes=(), pwrites=()):
        ch = self.ch_next
        self.ch_next = (self.ch_next + 1) % self.NCH
        waits = self._waits(q, reads, writes, pwrites)
        prev = self.ch_cnt[ch]
        key = ("c", ch, 0)
        if prev > 0 and self.seen[q].get(key, 0) < 16 * prev:
            self.seen[q][key] = 16 * prev
            waits.append((key, 16 * prev))
        self.ch_cnt[ch] += 1
        tok = (key, 16 * self.ch_cnt[ch])
        self.prog[q].append(("dma", "dma_start", kw, waits, tok))
        self._commit(tok, reads, writes, pwrites)
        self.n_instr += 1

    def barrier(self):
        toks = [t for t in self.last.values() if t is not None]
        toks += [(("c", ch, 0), 16 * n) for ch, n in enumerate(self.ch_cnt) if n > 0]
        for e in self.ENGS:
            waits = []
            for key, val in toks:
                if key[0] == "e" and key[1] == e:
                    continue
                if self.seen[e].get(key, 0) >= val:
                    continue
                self.seen[e][key] = val
                waits.append((key, val))
            self.prog[e].append(("wait", None, None, waits, None))

    def final_wait(self, q, bufs):
        waits = self._waits(q, bufs, (), ())
        self.prog[q].append(("wait", None, None, waits, None))

    def emit(self):
        nc = self.nc
        with nc.Block() as block:
            def run(e, eng):
                for kind, meth, kw, waits, tok in self.prog[e]:
                    for key, val in waits:
                        eng.wait_ge(self._sem(key), val)
                    if kind == "wait":
                        continue
                    ins = getattr(eng, meth)(**kw)
                    ins.then_inc(self._sem(tok[0]), 16 if kind == "dma" else 1)

            @block.sync
            def _(eng):
                run("sp", eng)

            @block.tensor
            def _(eng):
                run("pe", eng)

            @block.vector
            def _(eng):
                run("dve", eng)

            @block.scalar
            def _(eng):
                run("act", eng)

            @block.gpsimd
            def _(eng):
                run("pool", eng)


def build_layer(TL, NOL, debug=False):
    NT = TL // 128
    NF = NT + 2
    TO = CTX + NOL * 128
    NG = 2 + NOL + NT
    L0 = 2 + NOL
    NKT = TL + 768
    nc = bass.Bass("TRN2", target_bir_lowering=False)
    dt_in = lambda n, s: nc.dram_tensor(n, s, F32, kind="ExternalInput").ap()
    xl = dt_in("xl", [TL, D]); xo = dt_in("xo", [TO, D]); xh = dt_in("xh", [512, D])
    cvec = dt_in("cvec", [128, 32])
    w_mod = dt_in("w_mod", [D, 3 * D]); b_mod = dt_in("b_mod", [1, 3 * D]); norm_g = dt_in("norm_g", [1, D])
    w_in = dt_in("w_in", [D, NIN]); b_in = dt_in("b_in", [1, NIN])
    w_g = dt_in("w_g", [D, 32]); b_g = dt_in("b_g", [8, 4])
    a_norm_g = dt_in("a_norm_g", [1, D]); w_a = dt_in("w_a", [D, D]); w_b = dt_in("w_b", [D, D])
    qg = dt_in("qg", [1, D]); kg = dt_in("kg", [1, D])
    w_o = dt_in("w_o", [D, D]); b_o = dt_in("b_o", [1, D])
    nat = dt_in("nat", [5, 16, 128, 6, 128])
    rope_f = dt_in("rope_f", [NF * 128, 128]); rope_o = dt_in("rope_o", [max(NOL, 1) * 128, 128])
    gmask = dt_in("gmask", [4, max(NOL, 1) * 128])
    out_l = nc.dram_tensor("out_l", [TL, D], F32, kind="ExternalOutput").ap()
    out_c = nc.dram_tensor("out_c", [CTX, D], F32, kind="ExternalOutput").ap()
    di = lambda n, s, d: nc.dram_tensor(n, s, d, kind="Internal").ap()
    wb_in = di("wb_in", [D, NIN], BF16); wb_a = di("wb_a", [D, D], BF16); wb_b = di("wb_b", [D, D], BF16)
    wb_o = di("wb_o", [D, D], BF16); wb_mod = di("wb_mod", [D, 3 * D], BF16)
    modt = di("modt", [6, 128, D], F32)
    SPLIT = OFF["bk"]
    pl_fa = di("pl_fa", [NF * 128, SPLIT], F32); pl_fb = di("pl_fb", [NF * 128, NIN - SPLIT], F32)

    class _PLF:
        def __getitem__(self, idx):
            rs, cs = idx
            c0, c1 = cs.start, cs.stop
            if c0 >= SPLIT:
                return pl_fb[rs, c0 - SPLIT:c1 - SPLIT]
            assert c1 <= SPLIT, (c0, c1)
            return pl_fa[rs, c0:c1]
    pl_f = _PLF()
    pl_o = di("pl_o", [max(NOL, 1) * 128, 3072], F32)
    pl_h = di("pl_h", [512, 4096], F32)
    zrow = [di(f"zrow{d}", [8, NG * 128], F32) for d in range(2)]
    nbrow = [di(f"nbrow{d}", [8, NG * 128], F32) for d in range(2)]
    hf = di("hf", [NF * 128, D], F32)
    ya_d = di("ya_d", [NF * 128, D], BF16); yb_d = di("yb_d", [NF * 128, D], BF16)
    KT = di("KT", [NKT // 128, 128, 2048], BF16); VX = di("VX", [NKT, 16 * 129], BF16)

    with contextlib.ExitStack() as st0:
        S = Sched(nc, st0)
        psf = [st0.enter_context(nc.psum_tensor(f"psf{i}", [128, 512], F32)) for i in range(6)]
        psb = [st0.enter_context(nc.psum_tensor(f"psb{i}", [128, 1024], BF16)) for i in range(2)]
        Bpsf = [Buf(f"psf{i}") for i in range(6)]
        Bpsb = [Buf(f"psb{i}") for i in range(2)]
        Bwb_in = Buf(); Bwb_a = Buf(); Bwb_b = Buf(); Bwb_o = Buf(); Bwb_mod = Buf(); Bmodt = Buf()
        Bpl_f = [Buf() for _ in range(NF)]; Bpl_o = [Buf() for _ in range(max(NOL, 1))]; Bpl_h = [Buf() for _ in range(4)]
        Brow = [[Buf() for _ in range(NG)] for _ in range(2)]
        Bhf = [Buf() for _ in range(NF)]; Bya = [Buf() for _ in range(NF)]; Byb = [Buf() for _ in range(NF)]
        BKT = [Buf() for _ in range(NKT // 128)]; BVX = [Buf() for _ in range(NKT // 128)]
        Bout = Buf()

        class T:
            def __init__(self, stack, name, shape, dtype):
                self.t = stack.enter_context(nc.sbuf_tensor(name, shape, dtype))
                self.b = Buf(name)

        idf = T(st0, "idf", [128, 128], F32); idb = T(st0, "idb", [128, 128], BF16)
        trif = T(st0, "trif", [128, 128], F32); trib = T(st0, "trib", [128, 128], F32)
        S.op("pool", "memset", dict(ap=idf.t[:], constant=0.0), writes=[idf.b])
        S.op("pool", "affine_select", dict(out=idf.t[:], in_=idf.t[:], pattern=[[-1, 128]], compare_op=ALU.not_equal,
                                           fill=1.0, base=0, channel_multiplier=1), reads=[idf.b], writes=[idf.b])
        S.op("dve", "tensor_copy", dict(out=idb.t[:], in_=idf.t[:]), reads=[idf.b], writes=[idb.b])
        S.op("pool", "memset", dict(ap=trif.t[:], constant=1.0), writes=[trif.b])
        S.op("pool", "affine_select", dict(out=trif.t[:], in_=trif.t[:], pattern=[[1, 128]], compare_op=ALU.is_ge,
                                           fill=0.0, base=0, channel_multiplier=-1), reads=[trif.b], writes=[trif.b])
        S.op("pool", "memset", dict(ap=trib.t[:], constant=1.0), writes=[trib.b])
        S.op("pool", "affine_select", dict(out=trib.t[:], in_=trib.t[:], pattern=[[-1, 128]], compare_op=ALU.is_ge,
                                           fill=0.0, base=0, channel_multiplier=1), reads=[trib.b], writes=[trib.b])

        def cast_dram(dst, src, rows, cols, Bdst):
            for r0 in range(0, rows, 128):
                for c0 in range(0, cols, 2048):
                    w = min(2048, cols - c0)
                    S.dma("pool", dict(out=dst[r0:r0 + 128, c0:c0 + w], in_=src[r0:r0 + 128, c0:c0 + w]), pwrites=[Bdst])

        cast_dram(wb_mod, w_mod, D, 3 * D, Bwb_mod)
        cast_dram(wb_in, w_in, D, NIN, Bwb_in)
        cast_dram(wb_a, w_a, D, D, Bwb_a)
        cast_dram(wb_b, w_b, D, D, Bwb_b)
        cast_dram(wb_o, w_o, D, D, Bwb_o)

        with contextlib.ExitStack() as st:
            S.barrier()
            cv = T(st, "cv", [128, 32], F32); scb = T(st, "scb", [128, 32], BF16)
            screp = T(st, "screp", [128, 32, 128], BF16)
            wm = [T(st, f"wm{i}", [128, 16, 512], BF16) for i in range(2)]
            bm = T(st, "bm", [128, 3 * D], F32); gbc = T(st, "gbc", [128, D], F32)
            mod = T(st, "mod", [128, 3 * D], F32); a1 = T(st, "a1", [128, D], F32)
            S.dma("sp", dict(out=cv.t[:], in_=cvec[:, :]), writes=[cv.b])
            S.dma("sp", dict(out=bm.t[:], in_=b_mod[0:1, :].broadcast_to([128, 3 * D])), writes=[bm.b])
            S.dma("sp", dict(out=gbc.t[:], in_=norm_g[0:1, :].broadcast_to([128, D])), writes=[gbc.b])
            S.op("act", "activation", dict(out=scb.t[:], in_=cv.t[:], func=AF.Silu), reads=[cv.b], writes=[scb.b])
            S.op("dve", "tensor_copy", dict(out=screp.t[:], in_=scb.t[:].unsqueeze(2).broadcast_to([128, 32, 128])),
                 reads=[scb.b], writes=[screp.b])
            for v in range(2):
                for cb in range(12):
                    wt = wm[cb % 2]
                    S.dma("sp", dict(out=wt.t[:], in_=wb_mod[:, cb * 512:(cb + 1) * 512].rearrange("(k p) n -> p k n", p=128)),
                          reads=[Bwb_mod], writes=[wt.b])
                    pb = cb % 2
                    for k in range(16):
                        S.op("pe", "matmul", dict(out=psf[pb][:], lhsT=screp.t[:, v * 16 + k, :], rhs=wt.t[:, k, :],
                                                  start=(k == 0), stop=(k == 15)),
                             reads=[screp.b, wt.b], pwrites=[Bpsf[pb]])
                    S.op("dve", "tensor_tensor", dict(out=mod.t[:, cb * 512:(cb + 1) * 512], in0=psf[pb][:],
                                                      in1=bm.t[:, cb * 512:(cb + 1) * 512], op=ALU.add),
                         reads=[Bpsf[pb], bm.b], pwrites=[mod.b])
                S.op("dve", "scalar_tensor_tensor", dict(out=a1.t[:], in0=mod.t[:, D:2 * D], scalar=1.0, in1=gbc.t[:],
                                                         op0=ALU.add, op1=ALU.mult), reads=[mod.b, gbc.b], writes=[a1.b])
                S.dma("sp", dict(out=modt[3 * v + 0], in_=a1.t[:]), reads=[a1.b], pwrites=[Bmodt])
                S.dma("sp", dict(out=modt[3 * v + 1], in_=mod.t[:, 0:D]), reads=[mod.b], pwrites=[Bmodt])
                S.dma("sp", dict(out=modt[3 * v + 2], in_=mod.t[:, 2 * D:3 * D]), reads=[mod.b], pwrites=[Bmodt])

        Aarr = [T(st0, f"Aarr{d}", [8, NG], F32) for d in range(2)]
        ngarr = [T(st0, f"ngarr{d}", [8, NG], F32) for d in range(2)]

        with contextlib.ExitStack() as st:
            S.barrier()
            A1 = [T(st, f"A1_{v}", [128, D], F32) for v in range(2)]
            A0 = [T(st, f"A0_{v}", [128, D], F32) for v in range(2)]
            for v in range(2):
                S.dma("sp", dict(out=A1[v].t[:], in_=modt[3 * v + 0]), reads=[Bmodt], writes=[A1[v].b])
                S.dma("sp", dict(out=A0[v].t[:], in_=modt[3 * v + 1]), reads=[Bmodt], writes=[A0[v].b])
            wg = T(st, "wg", [128, 16, 32], BF16); wgf = T(st, "wgf", [128, 16, 32], F32); bg = T(st, "bg", [8, 4], F32)
            S.dma("sp", dict(out=wgf.t[:], in_=w_g.rearrange("(k p) n -> p k n", p=128)), writes=[wgf.b])
            S.op("dve", "tensor_copy", dict(out=wg.t[:], in_=wgf.t[:]), reads=[wgf.b], writes=[wg.b])
            S.dma("sp", dict(out=bg.t[:], in_=b_g[:, :]), writes=[bg.b])
            rm0 = T(st, "rm0", [8, 512], F32); rm1 = T(st, "rm1", [8, 512], F32)
            S.op("pool", "memset", dict(ap=rm0.t[:], constant=1.0), writes=[rm0.b])
            S.op("pool", "memset", dict(ap=rm1.t[:], constant=1.0), writes=[rm1.b])
            S.op("pool", "memset", dict(ap=rm0.t[:].rearrange("p (c t) -> p c t", t=128)[:, :, 0:1], constant=0.0), writes=[rm0.b])
            S.op("pool", "memset", dict(ap=rm1.t[:].rearrange("p (c t) -> p c t", t=128)[:, :, 127:128], constant=0.0), writes=[rm1.b])
            xt = [T(st, f"xt{i}", [128, D], F32) for i in range(2)]
            sq = T(st, "sq", [128, D], F32); hb_ = [T(st, f"hbf{i}", [128, D], BF16) for i in range(2)]
            st4 = [T(st, f"st4_{i}", [128, 4], F32) for i in range(2)]
            hT = [T(st, f"hT{i}", [128, 16, 512], BF16) for i in range(1)]
            wblk = [T(st, f"wblk{i}", [128, 16, 512], BF16) for i in range(2)]
            bblk = [T(st, f"bblk{i}", [128, 512], F32) for i in range(2)]
            stg = [T(st, f"stg{i}", [128, 512], F32) for i in range(2)]
            gI = T(st, "gI", [8, 512], F32); gF = T(st, "gF", [8, 512], F32); gL = T(st, "gL", [8, 512], F32)
            gnb = T(st, "gnb", [8, 512], F32); gz = T(st, "gz", [8, 512], F32); gmk = T(st, "gmk", [8, 512], F32)
            ALLB = [(c0, min(512, OFF["bk"] - c0)) for c0 in range(0, OFF["bk"], 512)] + [(c0, 512) for c0 in range(OFF["bk"], NIN, 512)]
            sets = [
                ("ctx", xo, 0, 2, 1, ALLB, 0, pl_f, 0, Bpl_f, 0, 0),
                ("loc", xl, 0, NT, 0, ALLB, 0, pl_f, 256, Bpl_f, 2, L0),
                ("ol", xo, 256, NOL, 0, [(OFF["ak"] + i * 512, 512) for i in range(6)], OFF["ak"], pl_o, 0, Bpl_o, 0, 2),
                ("halo", xh, 0, 4, 0, [(OFF["bk"] + i * 512, 512) for i in range(8)], OFF["bk"], pl_h, 0, Bpl_h, 0, None),
            ]
            grp = 0; evq = 0
            for (sname, xsrc, xoff, nch, mv, blocks, cbase, dst, droff, Bdst, bidx0, gc0) in sets:
                for g0 in range(0, nch, 4):
                    gn = min(4, nch - g0); W = gn * 128
                    hTt = hT[0]; grp += 1
                    for j in range(gn):
                        ci = g0 + j
                        x_ = xt[ci % 2]; hbt = hb_[ci % 2]; s4 = st4[ci % 2]
                        r0 = xoff + ci * 128
                        S.dma("sp", dict(out=x_.t[:], in_=xsrc[r0:r0 + 128, :]), writes=[x_.b])
                        S.op("act", "activation", dict(out=sq.t[:], in_=x_.t[:], func=AF.Square), reads=[x_.b], writes=[sq.b])
                        S.op("dve", "tensor_reduce", dict(out=s4.t[:, 0:1], in_=sq.t[:], axis=AX.X, op=ALU.add), reads=[sq.b], writes=[s4.b])
                        S.op("dve", "tensor_scalar", dict(out=s4.t[:, 1:2], in0=s4.t[:, 0:1], scalar1=1.0 / D, scalar2=EPS, op0=ALU.mult, op1=ALU.add),
                             reads=[s4.b], writes=[s4.b])
                        S.op("act", "activation", dict(out=s4.t[:, 2:3], in_=s4.t[:, 1:2], func=AF.Sqrt), reads=[s4.b], writes=[s4.b])
                        S.op("dve", "reciprocal", dict(out=s4.t[:, 3:4], in_=s4.t[:, 2:3]), reads=[s4.b], writes=[s4.b])
                        S.op("dve", "scalar_tensor_tensor", dict(out=sq.t[:], in0=x_.t[:], scalar=s4.t[:, 3:4], in1=A1[mv].t[:],
                                                                 op0=ALU.mult, op1=ALU.mult), reads=[x_.b, s4.b, A1[mv].b], writes=[sq.b])
                        S.op("pool", "tensor_tensor", dict(out=hbt.t[:], in0=sq.t[:], in1=A0[mv].t[:], op=ALU.add),
                             reads=[sq.b, A0[mv].b], writes=[hbt.b])
                        for k4 in range(4):
                            pb = k4 % 2
                            for kk in range(4):
                                k = k4 * 4 + kk
                                S.op("pe", "transpose", dict(out=psb[pb][:, kk * 128:(kk + 1) * 128], in_=hbt.t[:, k * 128:(k + 1) * 128],
                                                             identity=idb.t[:]), reads=[hbt.b, idb.b], pwrites=[Bpsb[pb]])
                            S.op("act", "activation", dict(out=hTt.t[:, k4 * 4:(k4 + 1) * 4, j * 128:(j + 1) * 128],
                                                           in_=psb[pb][:, 0:512].rearrange("p (a t) -> p a t", t=128), func=AF.Copy),
                                 reads=[Bpsb[pb]], pwrites=[hTt.b])
                    if gc0 is not None:
                        gc = gc0 + g0
                        for d in range(2):
                            for which, dstt in ((0, gI), (1, gF)):
                                col = (2 * d + which) * 8
                                for k in range(16):
                                    S.op("pe", "matmul", dict(out=psf[5][0:8, 0:W], lhsT=wg.t[:, k, col:col + 8], rhs=hTt.t[:, k, 0:W],
                                                              start=(k == 0), stop=(k == 15)), reads=[wg.b, hTt.b], pwrites=[Bpsf[5]])
                                S.op("act", "activation", dict(out=dstt.t[:, 0:W], in_=psf[5][0:8, 0:W], func=AF.Identity,
                                                               bias=bg.t[:, 2 * d + which:2 * d + which + 1]), reads=[Bpsf[5], bg.b], writes=[dstt.b])
                                if sname == "ol":
                                    S.dma("sp", dict(out=gmk.t[:, 0:W], in_=gmask[2 * d + which:2 * d + which + 1, g0 * 128:g0 * 128 + W].broadcast_to([8, W])),
                                          writes=[gmk.b])
                                    S.op("dve", "tensor_tensor", dict(out=dstt.t[:, 0:W], in0=dstt.t[:, 0:W], in1=gmk.t[:, 0:W], op=ALU.add),
                                         reads=[dstt.b, gmk.b], writes=[dstt.b])
                            S.op("act", "activation", dict(out=gL.t[:, 0:W], in_=gF.t[:, 0:W], func=AF.Exp, scale=-1.0), reads=[gF.b], writes=[gL.b])
                            S.op("act", "activation", dict(out=gL.t[:, 0:W], in_=gL.t[:, 0:W], func=AF.Ln, bias=1.0), reads=[gL.b], writes=[gL.b])
                            if d == 0:
                                S.op("dve", "tensor_tensor_scan", dict(out=gnb.t[:, 0:W], data0=rm0.t[:, 0:W], data1=gL.t[:, 0:W], initial=0.0,
                                                                       op0=ALU.mult, op1=ALU.add), reads=[rm0.b, gL.b], writes=[gnb.b])
                            else:
                                S.op("dve", "tensor_tensor_scan", dict(out=gnb.t[:, 0:W][:, ::-1], data0=rm1.t[:, 0:W][:, ::-1], data1=gL.t[:, 0:W][:, ::-1],
                                                                       initial=0.0, op0=ALU.mult, op1=ALU.add), reads=[rm1.b, gL.b], writes=[gnb.b])
                            S.op("dve", "tensor_tensor", dict(out=gz.t[:, 0:W], in0=gI.t[:, 0:W], in1=gnb.t[:, 0:W], op=ALU.add),
                                 reads=[gI.b, gnb.b], writes=[gz.b])
                            S.op("dve", "tensor_reduce", dict(out=Aarr[d].t[:, gc:gc + gn], in_=gz.t[:, 0:W].rearrange("p (c t) -> p c t", t=128),
                                                              axis=AX.X, op=ALU.max), reads=[gz.b], pwrites=[Aarr[d].b])
                            e_idx = 127 if d == 0 else 0
                            S.op("dve", "tensor_copy", dict(out=ngarr[d].t[:, gc:gc + gn],
                                                            in_=gnb.t[:, 0:W].rearrange("p (c t) -> p c t", t=128)[:, :, e_idx]),
                                 reads=[gnb.b], pwrites=[ngarr[d].b])
                            S.dma("sp", dict(out=zrow[d][:, gc * 128:gc * 128 + W], in_=gz.t[:, 0:W]), reads=[gz.b], pwrites=[Brow[d][gc + j] for j in range(gn)])
                            S.dma("sp", dict(out=nbrow[d][:, gc * 128:gc * 128 + W], in_=gnb.t[:, 0:W]), reads=[gnb.b], pwrites=[Brow[d][gc + j] for j in range(gn)])
                    for bi, (c0, w) in enumerate(blocks):
                        wt = wblk[bi % 2]; bt = bblk[bi % 2]
                        S.dma("sp", dict(out=wt.t[:, :, 0:w], in_=wb_in[:, c0:c0 + w].rearrange("(k p) n -> p k n", p=128)),
                              reads=[Bwb_in], writes=[wt.b])
                        S.dma("sp", dict(out=bt.t[:, 0:w], in_=b_in[0:1, c0:c0 + w].broadcast_to([128, w])), writes=[bt.b])
                        for j in range(gn):
                            pb = evq % 4; sg = stg[evq % 2]; evq += 1
                            for k in range(16):
                                S.op("pe", "matmul", dict(out=psf[pb][:, 0:w], lhsT=hTt.t[:, k, j * 128:(j + 1) * 128], rhs=wt.t[:, k, 0:w],
                                                          start=(k == 0), stop=(k == 15)), reads=[hTt.b, wt.b], pwrites=[Bpsf[pb]])
                            S.op("dve", "tensor_tensor", dict(out=sg.t[:, 0:w], in0=psf[pb][:, 0:w], in1=bt.t[:, 0:w], op=ALU.add),
                                 reads=[Bpsf[pb], bt.b], writes=[sg.b])
                            rr = droff + (g0 + j) * 128
                            S.dma("pool", dict(out=dst[rr:rr + 128, c0 - cbase:c0 - cbase + w], in_=sg.t[:, 0:w]), reads=[sg.b],
                                  pwrites=[Bdst[bidx0 + g0 + j]])

        S.barrier()
        mnew = [T(st0, f"mnew{d}", [8, NG], F32) for d in range(2)]
        mold = [T(st0, f"mold{d}", [8, NG], F32) for d in range(2)]
        Rr = [T(st0, f"Rr{d}", [8, NG], F32) for d in range(2)]
        nR = [T(st0, f"nR{d}", [8, NG], F32) for d in range(2)]
        dm = [T(st0, f"dm{d}", [8, NG], F32) for d in range(2)]

        def mscan(d, lo, hi, rev, init):
            o = mnew[d].t[:, lo:hi]; a = Aarr[d].t[:, lo:hi]; g = ngarr[d].t[:, lo:hi]
            if rev:
                o = o[:, ::-1]; a = a[:, ::-1]; g = g[:, ::-1]
            S.op("dve", "tensor_tensor_scan", dict(out=o, data0=a, data1=g, initial=init, op0=ALU.max, op1=ALU.subtract),
                 reads=[Aarr[d].b, ngarr[d].b, mnew[d].b], writes=[mnew[d].b])

        def cp(d, dst_lo, dst_hi, src_lo):
            n = dst_hi - dst_lo
            if n <= 0:
                return
            S.op("dve", "tensor_copy", dict(out=mold[d].t[:, dst_lo:dst_hi], in_=mnew[d].t[:, src_lo:src_lo + n]),
                 reads=[mnew[d].b, mold[d].b], writes=[mold[d].b])
        mscan(0, 0, NG, False, 0.0)
        S.op("dve", "memset", dict(ap=mold[0].t[:, 0:1], constant=0.0), reads=[mold[0].b], writes=[mold[0].b])
        cp(0, 1, NG, 0)
        mscan(1, 0, 2, True, 0.0)
        last = 0
        if NOL > 0:
            mscan(1, 2, L0, True, mnew[1].t[:, 0:1])
            last = 2
        mscan(1, L0, NG, True, mnew[1].t[:, last:last + 1])
        S.op("dve", "memset", dict(ap=mold[1].t[:, 1:2], constant=0.0), reads=[mold[1].b], writes=[mold[1].b])
        cp(1, 0, 1, 1)
        if NOL > 0:
            cp(1, L0 - 1, L0, 0)
            cp(1, 2, L0 - 1, 3)
        cp(1, NG - 1, NG, last)
        cp(1, L0, NG - 1, L0 + 1)
        for d in range(2):
            S.op("dve", "tensor_tensor", dict(out=Rr[d].t[:], in0=mold[d].t[:], in1=Aarr[d].t[:], op=ALU.max),
                 reads=[mold[d].b, Aarr[d].b], writes=[Rr[d].b])
            S.op("dve", "tensor_scalar", dict(out=nR[d].t[:], in0=Rr[d].t[:], scalar1=-1.0, scalar2=None, op0=ALU.mult),
                 reads=[Rr[d].b], writes=[nR[d].b])
            S.op("dve", "tensor_tensor", dict(out=dm[d].t[:], in0=mold[d].t[:], in1=Rr[d].t[:], op=ALU.subtract),
                 reads=[mold[d].b, Rr[d].b], writes=[dm[d].b])

        with contextlib.ExitStack() as st:
            S.barrier()
            Cst = T(st, "Cst", [128, 8, 257], F32); Cb = T(st, "Cb", [128, 8, 257], BF16)
            PK = [[T(st, f"PK{i}_{a}", [8, 128], F32) for a in range(4)] for i in range(2)]
            pkt = [T(st, f"pkt{i}", [128, 32], F32) for i in range(2)]
            zc = [T(st, f"zc{i}", [8, 128], F32) for i in range(2)]; nbc = [T(st, f"nbc{i}", [8, 128], F32) for i in range(2)]
            mu = T(st, "mu", [8, 128], F32); fl = T(st, "fl", [8, 128], F32); z8 = T(st, "z8", [8, 128], F32)
            kin = [T(st, f"kin{i}", [128, 1024], F32) for i in range(2)]; qin = [T(st, f"qin{i}", [128, 1024], F32) for i in range(2)]
            vin = [T(st, f"vin{i}", [128, 2048], F32) for i in range(2)]
            rp = [T(st, f"rp{i}", [128, 128], F32) for i in range(2)]
            t1 = T(st, "t1", [128, 512], F32); t2 = T(st, "t2", [128, 512], F32)
            kr = T(st, "kr", [128, 1024], F32); qr = T(st, "qr", [128, 1024], F32)
            ksb = T(st, "ksb", [128, 1024], BF16); kwb = T(st, "kwb", [128, 1024], BF16); qb = T(st, "qb", [128, 1024], BF16)
            vx = T(st, "vx", [128, 8, 257], BF16)
            kT = T(st, "kT", [128, 8, 128], BF16); qT = T(st, "qT", [128, 8, 128], BF16)
            pT_ = [T(st, f"pT{i}", [128, 128], BF16) for i in range(2)]
            hbuf = T(st, "hbuf", [128, 2048], F32); hfl = T(st, "hfl", [128, 2048], F32)
            sm = [T(st, f"sm{i}", [128, 8], F32) for i in range(2)]
            aob = T(st, "aob", [128, 2048], F32); azb = T(st, "azb", [128, 2048], F32)
            angb = T(st, "angb", [128, 2048], F32); yab = T(st, "yab", [128, 2048], BF16)
            hs8 = T(st, "hs8", [128, 32], F32)
            S.dma("sp", dict(out=angb.t[:], in_=a_norm_g[0:1, :].broadcast_to([128, D])), writes=[angb.b])
            S.op("pool", "memset", dict(ap=vx.t[:], constant=1.0), writes=[vx.b])
            S.op("pool", "memset", dict(ap=z8.t[:], constant=0.0), writes=[z8.b])
            for i in range(2):
                for a in range(4):
                    S.op("pool", "memset", dict(ap=PK[i][a].t[:], constant=0.0), writes=[PK[i][a].b])
            cnt = 0
            for d in range(2):
                S.op("pool", "memset", dict(ap=Cst.t[:], constant=0.0), reads=[Cst.b], writes=[Cst.b])
                tri = trif if d == 0 else trib
                if d == 0:
                    order = [("f", 0), ("f", 1)] + [("o", i) for i in range(NOL)] + [("f", 2 + j) for j in range(NT)]
                else:
                    order = [("f", 1), ("f", 0)] + [("o", i) for i in reversed(range(NOL))] + [("f", 2 + j) for j in reversed(range(NT))]
                for kind, idx in order:
                    full = kind == "f"
                    if full:
                        gc = idx if idx < 2 else L0 + idx - 2
                        ksrc = pl_f[idx * 128:(idx + 1) * 128, OFF["ak"]:OFF["ak"] + 1024]; Bk = Bpl_f[idx]
                        vsrc = pl_f[idx * 128:(idx + 1) * 128, OFF["av"]:OFF["av"] + 2048]
                        rsrc = rope_f[idx * 128:(idx + 1) * 128, :]
                    else:
                        gc = 2 + idx
                        ksrc = pl_o[idx * 128:(idx + 1) * 128, 0:1024]; Bk = Bpl_o[idx]
                        vsrc = pl_o[idx * 128:(idx + 1) * 128, 1024:3072]
                        rsrc = rope_o[idx * 128:(idx + 1) * 128, :]
                    i2 = cnt % 2; cnt += 1
                    pk = PK[i2]; pt = pkt[i2]; z_ = zc[i2]; nb_ = nbc[i2]; k_ = kin[i2]; v_ = vin[i2]; q_ = qin[i2]; r_ = rp[i2]
                    S.dma("sp", dict(out=z_.t[:], in_=zrow[d][:, gc * 128:(gc + 1) * 128]), reads=[Brow[d][gc]], writes=[z_.b])
                    S.dma("sp", dict(out=k_.t[:], in_=ksrc), reads=[Bk], writes=[k_.b])
                    S.dma("sp", dict(out=v_.t[:], in_=vsrc), reads=[Bk], writes=[v_.b])
                    S.dma("sp", dict(out=r_.t[:], in_=rsrc), writes=[r_.b])
                    S.op("act", "activation", dict(out=pk[0].t[:], in_=z_.t[:], func=AF.Exp, bias=nR[d].t[:, gc:gc + 1]),
                         reads=[z_.b, nR[d].b], writes=[pk[0].b])
                    S.op("act", "activation", dict(out=pk[3].t[:], in_=z8.t[:], func=AF.Exp, bias=dm[d].t[:, gc:gc + 1]),
                         reads=[z8.b, dm[d].b], writes=[pk[3].b])
                    if full:
                        S.dma("sp", dict(out=nb_.t[:], in_=nbrow[d][:, gc * 128:(gc + 1) * 128]), reads=[Brow[d][gc]], writes=[nb_.b])
                        S.dma("sp", dict(out=q_.t[:], in_=pl_f[idx * 128:(idx + 1) * 128, 0:1024]), reads=[Bk], writes=[q_.b])
                        zz = z_.t[:] if d == 0 else z_.t[:][:, ::-1]
                        mm = mu.t[:] if d == 0 else mu.t[:][:, ::-1]
                        S.op("dve", "tensor_tensor_scan", dict(out=mm, data0=zz, data1=zz, initial=mold[d].t[:, gc:gc + 1], op0=ALU.max, op1=ALU.max),
                             reads=[z_.b, mold[d].b], writes=[mu.b])
                        S.op("act", "activation", dict(out=pk[1].t[:], in_=mu.t[:], func=AF.Exp, scale=-1.0, bias=Rr[d].t[:, gc:gc + 1]),
                             reads=[mu.b, Rr[d].b], writes=[pk[1].b])
                        S.op("dve", "tensor_tensor", dict(out=fl.t[:], in0=nb_.t[:], in1=mu.t[:], op=ALU.subtract), reads=[nb_.b, mu.b], writes=[fl.b])
                        S.op("act", "activation", dict(out=pk[2].t[:], in_=fl.t[:], func=AF.Exp), reads=[fl.b], writes=[pk[2].b])
                    for a in range(4):
                        S.op("pe", "transpose", dict(out=psf[5][:, a * 8:(a + 1) * 8], in_=pk[a].t[:], identity=idf.t[0:8, 0:8]), reads=[pk[a].b, idf.b],
                             writes=[Bpsf[5]] if a == 0 else [], pwrites=[] if a == 0 else [Bpsf[5]])
                    S.op("dve", "tensor_copy", dict(out=pt.t[:], in_=psf[5][:, 0:32]), reads=[Bpsf[5]], writes=[pt.b])

                    def rope(src, dstt):
                        s5 = src.t[:].rearrange("p (h a u f) -> p h a u f", h=8, a=2, u=2, f=32)
                        d5 = dstt.t[:].rearrange("p (h a u f) -> p h a u f", h=8, a=2, u=2, f=32)
                        u1 = s5[:, :, :, 0, :]; u2 = s5[:, :, :, 1, :]
                        cosb = r_.t[:, 0:64].rearrange("p (a f) -> p a f", a=2).unsqueeze(1).broadcast_to([128, 8, 2, 32])
                        sinb = r_.t[:, 64:128].rearrange("p (a f) -> p a f", a=2).unsqueeze(1).broadcast_to([128, 8, 2, 32])
                        a_ = t1.t[:].rearrange("p (h a f) -> p h a f", h=8, a=2); b_ = t2.t[:].rearrange("p (h a f) -> p h a f", h=8, a=2)
                        S.op("dve", "tensor_tensor", dict(out=a_, in0=u1, in1=cosb, op=ALU.mult), reads=[src.b, r_.b], writes=[t1.b])
                        S.op("pool", "tensor_tensor", dict(out=b_, in0=u2, in1=sinb, op=ALU.mult), reads=[src.b, r_.b], writes=[t2.b])
                        S.op("dve", "tensor_tensor", dict(out=d5[:, :, :, 0, :], in0=a_, in1=b_, op=ALU.subtract), reads=[t1.b, t2.b], writes=[dstt.b])
                        S.op("dve", "tensor_tensor", dict(out=a_, in0=u1, in1=sinb, op=ALU.mult), reads=[src.b, r_.b], writes=[t1.b])
                        S.op("pool", "tensor_tensor", dict(out=b_, in0=u2, in1=cosb, op=ALU.mult), reads=[src.b, r_.b, dstt.b], writes=[t2.b])
                        S.op("dve", "tensor_tensor", dict(out=d5[:, :, :, 1, :], in0=a_, in1=b_, op=ALU.add), reads=[t1.b, t2.b], pwrites=[dstt.b])

                    rope(k_, kr)
                    S.op("act", "activation", dict(out=ksb.t[:], in_=kr.t[:], func=AF.Identity, scale=float(128 ** -0.5)), reads=[kr.b], writes=[ksb.b])
                    for h in range(8):
                        S.op("dve", "tensor_scalar", dict(out=kwb.t[:, h * 128:(h + 1) * 128], in0=ksb.t[:, h * 128:(h + 1) * 128],
                                                          scalar1=pt.t[:, h:h + 1], scalar2=None, op0=ALU.mult),
                             reads=[ksb.b, pt.b], writes=[kwb.b] if h == 0 else [], pwrites=[] if h == 0 else [kwb.b])
                    S.op("act", "activation", dict(out=vx.t[:, :, 0:256], in_=v_.t[:].rearrange("p (h v) -> p h v", h=8), func=AF.Copy),
                         reads=[v_.b], writes=[vx.b])
                    if full:
                        rope(q_, qr)
                        S.op("act", "activation", dict(out=qb.t[:], in_=qr.t[:], func=AF.Copy), reads=[qr.b], writes=[qb.b])
                        for (srcb, dstT) in ((ksb, kT), (qb, qT)):
                            for h4 in range(2):
                                for hh in range(4):
                                    h = h4 * 4 + hh
                                    S.op("pe", "transpose", dict(out=psb[h4][:, hh * 128:(hh + 1) * 128], in_=srcb.t[:, h * 128:(h + 1) * 128],
                                                                 identity=idb.t[:]), reads=[srcb.b, idb.b], pwrites=[Bpsb[h4]])
                                S.op("act", "activation", dict(out=dstT.t[:, h4 * 4:(h4 + 1) * 4, :],
                                                               in_=psb[h4][:, 0:512].rearrange("p (a t) -> p a t", t=128), func=AF.Copy),
                                     reads=[Bpsb[h4]], writes=[dstT.b] if h4 == 0 else [], pwrites=[] if h4 == 0 else [dstT.b])
                        S.op("dve", "tensor_scalar", dict(out=Cb.t[:].rearrange("p h v -> p (h v)"), in0=Cst.t[:].rearrange("p h v -> p (h v)"),
                                                          scalar1=1.0, scalar2=None, op0=ALU.mult), reads=[Cst.b], writes=[Cb.b])
                        for h in range(8):
                            S.op("dve", "tensor_scalar", dict(out=Cb.t[:, h, :], in0=Cst.t[:, h, :], scalar1=pt.t[:, 24 + h:25 + h], scalar2=None,
                                                              op0=ALU.mult), reads=[Cst.b, pt.b], pwrites=[Cb.b])
                    for h in range(8):
                        if full:
                            ps_s = h % 2; ps_o = 2 + h % 2
                            S.op("pe", "matmul", dict(out=psf[ps_s][:, 0:128], lhsT=kT.t[:, h, :], rhs=qT.t[:, h, :], start=True, stop=True),
                                 reads=[kT.b, qT.b], writes=[Bpsf[ps_s]])
                            pp = pT_[h % 2]
                            S.op("dve", "scalar_tensor_tensor", dict(out=pp.t[:], in0=psf[ps_s][:, 0:128], scalar=pt.t[:, h:h + 1], in1=tri.t[:],
                                                                     op0=ALU.mult, op1=ALU.mult), reads=[Bpsf[ps_s], pt.b, tri.b], writes=[pp.b])
                            S.op("pe", "matmul", dict(out=psf[ps_o][:, 0:257], lhsT=pp.t[:], rhs=vx.t[:, h, :], start=True, stop=False),
                                 reads=[pp.b, vx.b], writes=[Bpsf[ps_o]])
                            S.op("pe", "matmul", dict(out=psf[ps_o][:, 0:257], lhsT=qT.t[:, h, :], rhs=Cb.t[:, h, :], start=False, stop=True),
                                 reads=[qT.b, Cb.b], pwrites=[Bpsf[ps_o]])
                            s_ = sm[h % 2]
                            S.op("act", "activation", dict(out=s_.t[:, 0:1], in_=psf[ps_o][:, 256:257], func=AF.Abs, scale=pt.t[:, 8 + h:9 + h]),
                                 reads=[Bpsf[ps_o], pt.b], writes=[s_.b])
                            S.op("dve", "tensor_tensor", dict(out=s_.t[:, 1:2], in0=s_.t[:, 0:1], in1=pt.t[:, 16 + h:17 + h], op=ALU.max),
                                 reads=[s_.b, pt.b], writes=[s_.b])
                            S.op("dve", "reciprocal", dict(out=s_.t[:, 2:3], in_=s_.t[:, 1:2]), reads=[s_.b], writes=[s_.b])
                            S.op("dve", "tensor_tensor", dict(out=s_.t[:, 3:4], in0=s_.t[:, 2:3], in1=pt.t[:, 8 + h:9 + h], op=ALU.mult),
                                 reads=[s_.b, pt.b], writes=[s_.b])
                            S.op("act", "activation", dict(out=hbuf.t[:, h * 256:(h + 1) * 256], in_=psf[ps_o][:, 0:256], func=AF.Identity,
                                                           scale=s_.t[:, 3:4]), reads=[Bpsf[ps_o], s_.b],
                                 writes=[hbuf.b] if h == 0 else [], pwrites=[] if h == 0 else [hbuf.b])
                        ps_c = 4
                        S.op("pe", "matmul", dict(out=psf[ps_c][:, 0:257], lhsT=kwb.t[:, h * 128:(h + 1) * 128], rhs=vx.t[:, h, :], start=True, stop=True),
                             reads=[kwb.b, vx.b], writes=[Bpsf[ps_c]])
                        S.op("dve", "scalar_tensor_tensor", dict(out=Cst.t[:, h, :], in0=Cst.t[:, h, :], scalar=pt.t[:, 24 + h:25 + h], in1=psf[ps_c][:, 0:257],
                                                                 op0=ALU.mult, op1=ALU.add), reads=[Cst.b, pt.b, Bpsf[ps_c], Cb.b], pwrites=[Cst.b])
                    if full:
                        if d == 0:
                            S.dma("pool", dict(out=hf[idx * 128:(idx + 1) * 128, :], in_=hbuf.t[:]), reads=[hbuf.b], writes=[Bhf[idx]])
                        else:
                            S.dma("sp", dict(out=hfl.t[:], in_=hf[idx * 128:(idx + 1) * 128, :]), reads=[Bhf[idx]], writes=[hfl.b])
                            S.dma("sp", dict(out=aob.t[:], in_=pl_f[idx * 128:(idx + 1) * 128, OFF["ao"]:OFF["ao"] + 2048]), reads=[Bpl_f[idx]], writes=[aob.b])
                            S.dma("sp", dict(out=azb.t[:], in_=pl_f[idx * 128:(idx + 1) * 128, OFF["az"]:OFF["az"] + 2048]), reads=[Bpl_f[idx]], writes=[azb.b])
                            S.op("dve", "tensor_tensor", dict(out=hfl.t[:], in0=hfl.t[:], in1=hbuf.t[:], op=ALU.add), reads=[hfl.b, hbuf.b], writes=[hfl.b])
                            S.op("act", "activation", dict(out=hbuf.t[:], in_=hfl.t[:], func=AF.Square), reads=[hfl.b], writes=[hbuf.b])
                            S.op("dve", "tensor_reduce", dict(out=hs8.t[:, 0:8], in_=hbuf.t[:].rearrange("p (h v) -> p h v", h=8), axis=AX.X, op=ALU.add),
                                 reads=[hbuf.b], writes=[hs8.b])
                            S.op("dve", "tensor_scalar", dict(out=hs8.t[:, 8:16], in0=hs8.t[:, 0:8], scalar1=1.0 / 256, scalar2=EPS, op0=ALU.mult, op1=ALU.add),
                                 reads=[hs8.b], writes=[hs8.b])
                            S.op("act", "activation", dict(out=hs8.t[:, 16:24], in_=hs8.t[:, 8:16], func=AF.Sqrt), reads=[hs8.b], writes=[hs8.b])
                            S.op("dve", "reciprocal", dict(out=hs8.t[:, 24:32], in_=hs8.t[:, 16:24]), reads=[hs8.b], writes=[hs8.b])
                            S.op("dve", "tensor_tensor", dict(out=hfl.t[:].rearrange("p (h v) -> p h v", h=8), in0=hfl.t[:].rearrange("p (h v) -> p h v", h=8),
                                                              in1=hs8.t[:, 24:32].unsqueeze(2).broadcast_to([128, 8, 256]), op=ALU.mult),
                                 reads=[hfl.b, hs8.b], writes=[hfl.b])
                            S.op("pool", "tensor_tensor", dict(out=hfl.t[:], in0=hfl.t[:], in1=angb.t[:], op=ALU.mult), reads=[hfl.b, angb.b], writes=[hfl.b])
                            S.op("act", "activation", dict(out=aob.t[:], in_=aob.t[:], func=AF.Sigmoid), reads=[aob.b], writes=[aob.b])
                            S.op("act", "activation", dict(out=azb.t[:], in_=azb.t[:], func=AF.Silu), reads=[azb.b], writes=[azb.b])
                            S.op("dve", "tensor_tensor", dict(out=hfl.t[:], in0=hfl.t[:], in1=aob.t[:], op=ALU.mult), reads=[hfl.b, aob.b], writes=[hfl.b])
                            S.op("dve", "tensor_tensor", dict(out=yab.t[:], in0=hfl.t[:], in1=azb.t[:], op=ALU.mult), reads=[hfl.b, azb.b], writes=[yab.b])
                            S.dma("pool", dict(out=ya_d[idx * 128:(idx + 1) * 128, :], in_=yab.t[:]), reads=[yab.b], writes=[Bya[idx]])

        with contextlib.ExitStack() as st:
            S.barrier()
            kgb = T(st, "kgb", [128, D], F32)
            S.dma("sp", dict(out=kgb.t[:], in_=kg[0:1, :].broadcast_to([128, D])), writes=[kgb.b])
            kin_ = [T(st, f"nkin{i}", [128, D], F32) for i in range(2)]; vin_ = [T(st, f"nvin{i}", [128, D], F32) for i in range(2)]
            sq2 = T(st, "sq2", [128, D], F32); s16 = T(st, "s16", [128, 64], F32)
            knb = T(st, "knb", [128, D], BF16); ktile = T(st, "ktile", [128, 16, 128], BF16)
            vxt = T(st, "vxt", [128, 16, 129], BF16)
            S.op("pool", "memset", dict(ap=vxt.t[:], constant=1.0), writes=[vxt.b])

            def rms_heads(src, gains, dstb, extra_scale, sq2, s16):
                S.op("act", "activation", dict(out=sq2.t[:], in_=src.t[:], func=AF.Square), reads=[src.b], writes=[sq2.b])
                S.op("dve", "tensor_reduce", dict(out=s16.t[:, 0:16], in_=sq2.t[:].rearrange("p (h v) -> p h v", h=16), axis=AX.X, op=ALU.add),
                     reads=[sq2.b], writes=[s16.b])
                S.op("dve", "tensor_scalar", dict(out=s16.t[:, 16:32], in0=s16.t[:, 0:16], scalar1=1.0 / 128, scalar2=EPS, op0=ALU.mult, op1=ALU.add),
                     reads=[s16.b], writes=[s16.b])
                S.op("act", "activation", dict(out=s16.t[:, 32:48], in_=s16.t[:, 16:32], func=AF.Sqrt), reads=[s16.b], writes=[s16.b])
                S.op("dve", "reciprocal", dict(out=s16.t[:, 48:64], in_=s16.t[:, 32:48]), reads=[s16.b], writes=[s16.b])
                S.op("dve", "tensor_tensor", dict(out=sq2.t[:].rearrange("p (h v) -> p h v", h=16), in0=src.t[:].rearrange("p (h v) -> p h v", h=16),
                                                  in1=s16.t[:, 48:64].unsqueeze(2).broadcast_to([128, 16, 128]), op=ALU.mult),
                     reads=[src.b, s16.b], writes=[sq2.b])
                S.op("dve", "scalar_tensor_tensor", dict(out=dstb.t[:], in0=sq2.t[:], scalar=float(extra_scale), in1=gains.t[:], op0=ALU.mult, op1=ALU.mult),
                     reads=[sq2.b, gains.b], writes=[dstb.b])

            def transpose16(srcb, dstT):
                for h4 in range(4):
                    pb = h4 % 2
                    for hh in range(4):
                        h = h4 * 4 + hh
                        S.op("pe", "transpose", dict(out=psb[pb][:, hh * 128:(hh + 1) * 128], in_=srcb.t[:, h * 128:(h + 1) * 128], identity=idb.t[:]),
                             reads=[srcb.b, idb.b], pwrites=[Bpsb[pb]])
                    S.op("act", "activation", dict(out=dstT.t[:, h4 * 4:(h4 + 1) * 4, :], in_=psb[pb][:, 0:512].rearrange("p (a t) -> p a t", t=128), func=AF.Copy),
                         reads=[Bpsb[pb]], writes=[dstT.b] if h4 == 0 else [], pwrites=[] if h4 == 0 else [dstT.b])

            for kc in range(NKT // 128):
                if kc < 2:
                    src, r0, c_k, c_v, Bs = pl_f, kc * 128, OFF["bk"], OFF["bv"], Bpl_f[kc]
                elif kc < 4:
                    src, r0, c_k, c_v, Bs = pl_h, (kc - 2) * 128, 0, 2048, Bpl_h[kc - 2]
                elif kc < 4 + NT:
                    src, r0, c_k, c_v, Bs = pl_f, 256 + (kc - 4) * 128, OFF["bk"], OFF["bv"], Bpl_f[2 + kc - 4]
                else:
                    src, r0, c_k, c_v, Bs = pl_h, 256 + (kc - 4 - NT) * 128, 0, 2048, Bpl_h[2 + kc - 4 - NT]
                k_ = kin_[kc % 2]; v_ = vin_[kc % 2]
                S.dma("sp", dict(out=k_.t[:], in_=src[r0:r0 + 128, c_k:c_k + 2048]), reads=[Bs], writes=[k_.b])
                S.dma("sp", dict(out=v_.t[:], in_=src[r0:r0 + 128, c_v:c_v + 2048]), reads=[Bs], writes=[v_.b])
                rms_heads(k_, kgb, knb, 1.0, sq2, s16)
                transpose16(knb, ktile)
                S.dma("sp", dict(out=KT[kc], in_=ktile.t[:].rearrange("p h t -> p (h t)")), reads=[ktile.b], writes=[BKT[kc]])
                S.op("act", "activation", dict(out=vxt.t[:, :, 0:128], in_=v_.t[:].rearrange("p (h v) -> p h v", h=16), func=AF.Copy),
                     reads=[v_.b], writes=[vxt.b])
                S.dma("pool", dict(out=VX[kc * 128:(kc + 1) * 128, :], in_=vxt.t[:].rearrange("p h v -> p (h v)")), reads=[vxt.b], writes=[BVX[kc]])

        with contextlib.ExitStack() as st:
            S.barrier()
            qgb = T(st, "qgb", [128, D], F32)
            S.dma("sp", dict(out=qgb.t[:], in_=qg[0:1, :].broadcast_to([128, D])), writes=[qgb.b])
            sq2 = T(st, "sq2b", [128, D], F32); s16 = T(st, "s16b", [128, 64], F32)
            KTc = T(st, "KTc", [128, 2, 2048], BF16); VXc = T(st, "VXc", [128, 2, 16 * 129], BF16)
            S.dma("sp", dict(out=KTc.t[:], in_=KT[0:2].rearrange("c p n -> p c n")), reads=[BKT[0], BKT[1]], writes=[KTc.b])
            S.dma("sp", dict(out=VXc.t[:], in_=VX[0:256, :].rearrange("(c p) n -> p c n", p=128)), reads=[BVX[0], BVX[1]], writes=[VXc.b])
            KTb = [T(st, f"KTb{i}", [128, 6, 2048], BF16) for i in range(1)]
            VXb = [T(st, f"VXb{i}", [128, 6, 16 * 129], BF16) for i in range(1)]
            qin_ = [T(st, f"nqin{i}", [128, D], F32) for i in range(1)]; bzb = [T(st, f"bzb{i}", [128, D], F32) for i in range(1)]
            qnb = T(st, "qnb", [128, D], BF16); QT = T(st, "QT", [128, 16, 128], BF16)
            tab = [T(st, f"tab{i}", [128, 6, 128], F32) for i in range(2)]
            sx = [T(st, f"sx{i}", [128, 6, 128], F32) for i in range(2)]
            pt_ = [T(st, f"npt{i}", [128, 8, 128], BF16) for i in range(2)]
            ob = T(st, "ob", [128, D], F32); r1 = [T(st, f"r1_{i}", [128, 2], F32) for i in range(2)]
            ybb = T(st, "ybb", [128, D], BF16)
            hcnt = 0
            for fi in range(NF):
                if fi < 2:
                    pairs = []; var = None
                else:
                    j = fi - 2
                    if j == 0:
                        plo, np_, var = -2, 6, 0
                    elif j == NT - 1:
                        plo, np_, var = NT - 4, 6, 4
                    else:
                        plo, np_ = j - 2, 5
                        var = 1 if j == 1 else (3 if j == NT - 2 else 2)
                    pairs = list(range(plo, plo + np_))
                i2 = 0
                q_ = qin_[i2]; bz_ = bzb[i2]
                S.dma("sp", dict(out=q_.t[:], in_=pl_f[fi * 128:(fi + 1) * 128, OFF["bq"]:OFF["bq"] + 2048]), reads=[Bpl_f[fi]], writes=[q_.b])
                S.dma("sp", dict(out=bz_.t[:], in_=pl_f[fi * 128:(fi + 1) * 128, OFF["bz"]:OFF["bz"] + 2048]), reads=[Bpl_f[fi]], writes=[bz_.b])
                if pairs:
                    ktb = KTb[i2]; vxb = VXb[i2]
                    kt0 = 512 + pairs[0] * 128; nk = len(pairs) * 128
                    S.dma("sp", dict(out=ktb.t[:, 0:len(pairs), :], in_=KT[kt0 // 128:kt0 // 128 + len(pairs)].rearrange("c p n -> p c n")), reads=[BKT[kt0 // 128 + i] for i in range(len(pairs))], writes=[ktb.b])
                    S.dma("sp", dict(out=vxb.t[:, 0:len(pairs), :], in_=VX[kt0:kt0 + nk, :].rearrange("(c p) n -> p c n", p=128)),
                          reads=[BVX[kt0 // 128 + i] for i in range(len(pairs))], writes=[vxb.b])
                rms_heads(q_, qgb, qnb, 128 ** -0.5, sq2, s16)
                transpose16(qnb, QT)
                if debug and fi == 0:
                    dbg_qt = nc.dram_tensor("dbg_qt", [128, 16 * 128], BF16, kind="ExternalOutput").ap()
                    dbg_qn = nc.dram_tensor("dbg_qn", [128, D], BF16, kind="ExternalOutput").ap()
                    S.dma("pool", dict(out=dbg_qt[:, :], in_=QT.t[:].rearrange("p h t -> p (h t)")), reads=[QT.b], pwrites=[Bout])
                    S.dma("pool", dict(out=dbg_qn[:, :], in_=qnb.t[:]), reads=[qnb.b], pwrites=[Bout])
                S.op("act", "activation", dict(out=bz_.t[:], in_=bz_.t[:], func=AF.Silu), reads=[bz_.b], writes=[bz_.b])
                npair = len(pairs)
                for h in range(16):
                    hi = hcnt % 2; hcnt += 1
                    tb = tab[hi]; sx_ = sx[hi]; pp = pt_[hi]
                    nchunk = npair + 2
                    if npair:
                        S.dma("sp", dict(out=tb.t[:, 0:npair, :], in_=nat[var, h, :, 0:npair, :]), writes=[tb.b])
                    for c in range(nchunk):
                        bank = (hi * 2) + c // 4; slot = c % 4
                        if c < npair:
                            lhs = ktb.t[:, c, h * 128:(h + 1) * 128]; rd = [ktb.b, QT.b]
                        else:
                            lhs = KTc.t[:, c - npair, h * 128:(h + 1) * 128]; rd = [KTc.b, QT.b]
                        S.op("pe", "matmul", dict(out=psf[bank][:, slot * 128:(slot + 1) * 128], lhsT=lhs, rhs=QT.t[:, h, :], start=True, stop=True),
                             reads=rd, writes=[Bpsf[bank]] if slot == 0 else [], pwrites=[] if slot == 0 else [Bpsf[bank]])
                    first = True
                    for c0 in range(0, nchunk, 4):
                        bank = (hi * 2) + c0 // 4; n_here = min(4, nchunk - c0)
                        nl = max(0, min(npair, c0 + n_here) - c0)
                        if nl:
                            S.op("dve", "tensor_tensor", dict(out=sx_.t[:, c0:c0 + nl, :], in0=psf[bank][:, 0:nl * 128].rearrange("p (c t) -> p c t", t=128),
                                                              in1=tb.t[:, c0:c0 + nl, :], op=ALU.add), reads=[Bpsf[bank], tb.b],
                                 writes=[sx_.b] if c0 == 0 else [], pwrites=[] if c0 == 0 else [sx_.b])
                            S.op("act", "activation", dict(out=pp.t[:, c0:c0 + nl, :], in_=sx_.t[:, c0:c0 + nl, :], func=AF.Exp), reads=[sx_.b],
                                 writes=[pp.b] if first else [], pwrites=[] if first else [pp.b])
                            first = False
                        if n_here > nl:
                            S.op("act", "activation", dict(out=pp.t[:, c0 + nl:c0 + n_here, :],
                                                           in_=psf[bank][:, nl * 128:n_here * 128].rearrange("p (c t) -> p c t", t=128), func=AF.Exp),
                                 reads=[Bpsf[bank]], writes=[pp.b] if first else [], pwrites=[] if first else [pp.b])
                            first = False
                    po = 4 + hi
                    for c in range(nchunk):
                        if c < npair:
                            rhs = vxb.t[:, c, h * 129:(h + 1) * 129]; rd = [pp.b, vxb.b]
                        else:
                            rhs = VXc.t[:, c - npair, h * 129:(h + 1) * 129]; rd = [pp.b, VXc.b]
                        S.op("pe", "matmul", dict(out=psf[po][:, 0:129], lhsT=pp.t[:, c, :], rhs=rhs, start=(c == 0), stop=(c == nchunk - 1)),
                             reads=rd, writes=[Bpsf[po]] if c == 0 else [], pwrites=[] if c == 0 else [Bpsf[po]])
                    r_ = r1[hi]
                    S.op("dve", "reciprocal", dict(out=r_.t[:, 0:1], in_=psf[po][:, 128:129]), reads=[Bpsf[po]], writes=[r_.b])
                    S.op("act", "activation", dict(out=ob.t[:, h * 128:(h + 1) * 128], in_=psf[po][:, 0:128], func=AF.Identity, scale=r_.t[:, 0:1]),
                         reads=[Bpsf[po], r_.b], writes=[ob.b] if h == 0 else [], pwrites=[] if h == 0 else [ob.b])
                S.op("dve", "tensor_tensor", dict(out=ybb.t[:], in0=ob.t[:], in1=bz_.t[:], op=ALU.mult), reads=[ob.b, bz_.b], writes=[ybb.b])
                S.dma("pool", dict(out=yb_d[fi * 128:(fi + 1) * 128, :], in_=ybb.t[:]), reads=[ybb.b], writes=[Byb[fi]])

        with contextlib.ExitStack() as st:
            S.barrier()
            gate = [T(st, f"gate{v}", [128, D], F32) for v in range(2)]
            for v in range(2):
                S.dma("sp", dict(out=gate[v].t[:], in_=modt[3 * v + 2]), reads=[Bmodt], writes=[gate[v].b])
            bob = T(st, "bob", [128, D], F32)
            S.dma("sp", dict(out=bob.t[:], in_=b_o[0:1, :].broadcast_to([128, D])), writes=[bob.b])
            yin = [T(st, f"yin{i}", [128, D], BF16) for i in range(2)]
            yaT = T(st, "yaT", [128, 16, 512], BF16); ybT = T(st, "ybT", [128, 16, 512], BF16); yT = T(st, "yT", [128, 16, 512], BF16)
            wbk = [T(st, f"wbk{i}", [128, 16, 512], BF16) for i in range(2)]
            gsl = [T(st, f"gsl{i}", [128, 512], F32) for i in range(2)]
            ua = [T(st, f"ua{i}", [128, 512], F32) for i in range(2)]
            ysb = [T(st, f"ysb{i}", [128, D], BF16) for i in range(4)]
            xres = [T(st, f"xres{i}", [128, 512], F32) for i in range(2)]
            osb = [T(st, f"osb{i}", [128, 512], F32) for i in range(2)]
            groups = [(0, 2, 1)] + [(2 + g0, min(4, NT - g0), 0) for g0 in range(0, NT, 4)]
            wcnt = 0; ecnt = 0
            for (f0, gn, mv) in groups:
                for (srcd, Bsrc, dstT) in ((ya_d, Bya, yaT), (yb_d, Byb, ybT)):
                    for j in range(gn):
                        fi = f0 + j
                        yi = yin[(fi) % 2]
                        S.dma("sp", dict(out=yi.t[:], in_=srcd[fi * 128:(fi + 1) * 128, :]), reads=[Bsrc[fi]], writes=[yi.b])
                        for k4 in range(4):
                            pb = k4 % 2
                            for kk in range(4):
                                k = k4 * 4 + kk
                                S.op("pe", "transpose", dict(out=psb[pb][:, kk * 128:(kk + 1) * 128], in_=yi.t[:, k * 128:(k + 1) * 128], identity=idb.t[:]),
                                     reads=[yi.b, idb.b], pwrites=[Bpsb[pb]])
                            S.op("act", "activation", dict(out=dstT.t[:, k4 * 4:(k4 + 1) * 4, j * 128:(j + 1) * 128],
                                                           in_=psb[pb][:, 0:512].rearrange("p (a t) -> p a t", t=128), func=AF.Copy),
                                 reads=[Bpsb[pb]], pwrites=[dstT.b])
                for cb in range(4):
                    for (wsrc, Bw, srcT, goff, first) in ((wb_a, Bwb_a, yaT, OFF["ga"], True), (wb_b, Bwb_b, ybT, OFF["gb"], False)):
                        wt = wbk[wcnt % 2]; wcnt += 1
                        S.dma("sp", dict(out=wt.t[:], in_=wsrc[:, cb * 512:(cb + 1) * 512].rearrange("(k p) n -> p k n", p=128)), reads=[Bw], writes=[wt.b])
                        for j in range(gn):
                            fi = f0 + j
                            pb = ecnt % 4; gs = gsl[ecnt % 2]; u_ = ua[j % 2] if False else ua[ecnt % 2]; ecnt += 1
                            for k in range(16):
                                S.op("pe", "matmul", dict(out=psf[pb][:], lhsT=srcT.t[:, k, j * 128:(j + 1) * 128], rhs=wt.t[:, k, :], start=(k == 0), stop=(k == 15)),
                                     reads=[srcT.b, wt.b], pwrites=[Bpsf[pb]])
                            S.dma("sp", dict(out=gs.t[:], in_=pl_f[fi * 128:(fi + 1) * 128, goff + cb * 512:goff + (cb + 1) * 512]), reads=[Bpl_f[fi]], writes=[gs.b])
                            S.op("act", "activation", dict(out=gs.t[:], in_=gs.t[:], func=AF.Sigmoid), reads=[gs.b], writes=[gs.b])
                            if first:
                                S.op("dve", "tensor_tensor", dict(out=ysb[j].t[:, cb * 512:(cb + 1) * 512], in0=psf[pb][:], in1=gs.t[:], op=ALU.mult),
                                     reads=[Bpsf[pb], gs.b], pwrites=[ysb[j].b])
                            else:
                                S.op("dve", "tensor_tensor", dict(out=u_.t[:], in0=psf[pb][:], in1=gs.t[:], op=ALU.mult), reads=[Bpsf[pb], gs.b], writes=[u_.b])
                                S.op("pool", "tensor_tensor", dict(out=ysb[j].t[:, cb * 512:(cb + 1) * 512], in0=ysb[j].t[:, cb * 512:(cb + 1) * 512], in1=u_.t[:], op=ALU.add),
                                     reads=[u_.b, ysb[j].b], pwrites=[ysb[j].b])
                for j in range(gn):
                    for k4 in range(4):
                        pb = k4 % 2
                        for kk in range(4):
                            k = k4 * 4 + kk
                            S.op("pe", "transpose", dict(out=psb[pb][:, kk * 128:(kk + 1) * 128], in_=ysb[j].t[:, k * 128:(k + 1) * 128], identity=idb.t[:]),
                                 reads=[ysb[j].b, idb.b], pwrites=[Bpsb[pb]])
                        S.op("act", "activation", dict(out=yT.t[:, k4 * 4:(k4 + 1) * 4, j * 128:(j + 1) * 128],
                                                       in_=psb[pb][:, 0:512].rearrange("p (a t) -> p a t", t=128), func=AF.Copy),
                             reads=[Bpsb[pb]], pwrites=[yT.b])
                for cb in range(4):
                    wt = wbk[wcnt % 2]; wcnt += 1
                    S.dma("sp", dict(out=wt.t[:], in_=wb_o[:, cb * 512:(cb + 1) * 512].rearrange("(k p) n -> p k n", p=128)), reads=[Bwb_o], writes=[wt.b])
                    for j in range(gn):
                        fi = f0 + j
                        pb = ecnt % 4; xr_ = xres[ecnt % 2]; o_ = osb[ecnt % 2]; ecnt += 1
                        for k in range(16):
                            S.op("pe", "matmul", dict(out=psf[pb][:], lhsT=yT.t[:, k, j * 128:(j + 1) * 128], rhs=wt.t[:, k, :], start=(k == 0), stop=(k == 15)),
                                 reads=[yT.b, wt.b], pwrites=[Bpsf[pb]])
                        if fi < 2:
                            xs = xo[fi * 128:(fi + 1) * 128, cb * 512:(cb + 1) * 512]; od = out_c[fi * 128:(fi + 1) * 128, cb * 512:(cb + 1) * 512]
                        else:
                            xs = xl[(fi - 2) * 128:(fi - 1) * 128, cb * 512:(cb + 1) * 512]; od = out_l[(fi - 2) * 128:(fi - 1) * 128, cb * 512:(cb + 1) * 512]
                        S.dma("sp", dict(out=xr_.t[:], in_=xs), writes=[xr_.b])
                        S.op("dve", "tensor_tensor", dict(out=o_.t[:], in0=psf[pb][:], in1=bob.t[:, cb * 512:(cb + 1) * 512], op=ALU.add),
                             reads=[Bpsf[pb], bob.b], writes=[o_.b])
                        S.op("pool", "tensor_tensor", dict(out=o_.t[:], in0=o_.t[:], in1=gate[mv].t[:, cb * 512:(cb + 1) * 512], op=ALU.mult),
                             reads=[o_.b, gate[mv].b], writes=[o_.b])
                        S.op("dve", "tensor_tensor", dict(out=o_.t[:], in0=o_.t[:], in1=xr_.t[:], op=ALU.add), reads=[o_.b, xr_.b], writes=[o_.b])
                        S.dma("pool", dict(out=od, in_=o_.t[:]), reads=[o_.b], pwrites=[Bout])
        if debug:
            dbg_pl = nc.dram_tensor("dbg_pl", [NF * 128, NIN], F32, kind="ExternalOutput").ap()
            dbg_ya = nc.dram_tensor("dbg_ya", [NF * 128, D], F32, kind="ExternalOutput").ap()
            dbg_yb = nc.dram_tensor("dbg_yb", [NF * 128, D], F32, kind="ExternalOutput").ap()
            dbg_hf = nc.dram_tensor("dbg_hf", [NF * 128, D], F32, kind="ExternalOutput").ap()
            dbg_z = nc.dram_tensor("dbg_z", [4, 8, NG * 128], F32, kind="ExternalOutput").ap()
            dbg_kt = nc.dram_tensor("dbg_kt", [NKT // 128, 128, 2048], F32, kind="ExternalOutput").ap()
            dbg_vx = nc.dram_tensor("dbg_vx", [NKT, 16 * 129], F32, kind="ExternalOutput").ap()
            for kc in range(NKT // 128):
                S.dma("pool", dict(out=dbg_kt[kc], in_=KT[kc]), reads=BKT, pwrites=[Bout])
            for kc in range(NKT // 128):
                S.dma("pool", dict(out=dbg_vx[kc * 128:(kc + 1) * 128, 0:2048], in_=VX[kc * 128:(kc + 1) * 128, 0:2048]), reads=BVX, pwrites=[Bout])
                S.dma("pool", dict(out=dbg_vx[kc * 128:(kc + 1) * 128, 2048:2064], in_=VX[kc * 128:(kc + 1) * 128, 2048:2064]), reads=BVX, pwrites=[Bout])
            for fi in range(NF):
                segs = [(c0, min(2048, OFF["bk"] - c0)) for c0 in range(0, OFF["bk"], 2048)] + [(c0, 2048) for c0 in range(OFF["bk"], NIN, 2048)]
                for c0, w in segs:
                    S.dma("pool", dict(out=dbg_pl[fi * 128:(fi + 1) * 128, c0:c0 + w], in_=pl_f[fi * 128:(fi + 1) * 128, c0:c0 + w]), reads=[Bpl_f[fi]], pwrites=[Bout])
                S.dma("pool", dict(out=dbg_ya[fi * 128:(fi + 1) * 128, :], in_=ya_d[fi * 128:(fi + 1) * 128, :]), reads=[Bya[fi]], pwrites=[Bout])
                S.dma("pool", dict(out=dbg_yb[fi * 128:(fi + 1) * 128, :], in_=yb_d[fi * 128:(fi + 1) * 128, :]), reads=[Byb[fi]], pwrites=[Bout])
                S.dma("pool", dict(out=dbg_hf[fi * 128:(fi + 1) * 128, :], in_=hf[fi * 128:(fi + 1) * 128, :]), reads=[Bhf[fi]], pwrites=[Bout])
            for d in range(2):
                S.dma("pool", dict(out=dbg_z[d], in_=zrow[d][:, :]), reads=Brow[d], pwrites=[Bout])
                S.dma("pool", dict(out=dbg_z[2 + d], in_=nbrow[d][:, :]), reads=Brow[d], pwrites=[Bout])
        S.final_wait("pool", [Bout])
        S.emit()
        n_instr = S.n_instr
    return nc, n_instr


def _rope_rows(pos_tok, is_ctx):
    n = len(pos_tok)
    nf = 32
    inv = (10000.0 ** (-np.arange(nf, dtype=np.float32) / nf)).astype(np.float32)
    row = (pos_tok // GRID_W).astype(np.float32)[:, None] * inv
    col = (pos_tok % GRID_W).astype(np.float32)[:, None] * inv
    out = np.empty((n, 128), np.float32)
    out[:, 0:32] = np.cos(row); out[:, 32:64] = np.cos(col)
    out[:, 64:96] = np.sin(row); out[:, 96:128] = np.sin(col)
    if is_ctx is not None:
        out[is_ctx, 0:64] = 1.0
        out[is_ctx, 64:128] = 0.0
    return out


def _na_tables(rpb_l, s, NT, rows):
    tab = np.full((5, 16, 128, 6, 128), NEG, np.float32)
    variants = [(0, 0, -2, 6), (1, 1, -1, 5), (2, min(2, NT - 1), min(2, NT - 1) - 2, 5), (3, NT - 2, NT - 4, 5), (4, NT - 1, NT - 4, 6)]
    kk = np.arange(128); q = np.arange(128)
    for v, j, plo, npair in variants:
        J = s * NT + j
        r = 2 * J + q // 64; qc = q % 64
        rs = np.clip(r - 4, 0, rows - 8); ws = np.clip(qc - 8, 0, GRID_W - 16)
        for ps in range(npair):
            P = s * NT + plo + ps
            krow = 2 * P + kk // 64; kcol = kk % 64
            valid = ((krow[:, None] >= rs[None, :]) & (krow[:, None] < rs[None, :] + 8) &
                     (kcol[:, None] >= ws[None, :]) & (kcol[:, None] < ws[None, :] + 16) &
                     (krow[:, None] >= 0) & (krow[:, None] < rows))
            dr = np.clip(krow[:, None] - r[None, :] + 7, 0, 14); dc = np.clip(kcol[:, None] - qc[None, :] + 15, 0, 30)
            g = rpb_l[:, dr, dc]
            tab[v, :, :, ps, :] = np.where(valid[None], g, np.float32(NEG))
    return tab


_PROG = {}
_DEBUG_HOOK = None


def _get_prog(TL, NOL):
    key = (TL, NOL)
    if key not in _PROG:
        _PROG[key] = build_layer(TL, NOL)[0]
    return _PROG[key]


def kernel(x, c, ctx, c_ctx, w_mod, b_mod, norm_g, w_in, b_in, a_norm_g, w_br_a, w_br_b, na_q_g, na_k_g, na_rpb, w_out, b_out):
    x = np.asarray(x, np.float32); ctx = np.asarray(ctx, np.float32)
    B, T, _ = x.shape
    depth = w_mod.shape[0]
    NS = 8 // B
    TL = T // NS; NT = TL // 128
    NOL = (T - TL) // 128
    rows = T // GRID_W
    nc = _get_prog(TL, NOL)
    gcols = np.concatenate([np.arange(OFF["g"] + 8 * i, OFF["g"] + 8 * i + 8) for i in range(4)])
    for l in range(depth):
        in_maps = []
        wl = dict(
            w_mod=np.ascontiguousarray(w_mod[l]), b_mod=np.ascontiguousarray(b_mod[l][None]), norm_g=np.ascontiguousarray(norm_g[l][None]),
            w_in=np.ascontiguousarray(w_in[l]), b_in=np.ascontiguousarray(b_in[l][None]),
            w_g=np.ascontiguousarray(w_in[l][:, gcols]), b_g=np.ascontiguousarray(b_in[l][gcols].reshape(4, 8).T),
            a_norm_g=np.ascontiguousarray(a_norm_g[l][None]), w_a=np.ascontiguousarray(w_br_a[l]), w_b=np.ascontiguousarray(w_br_b[l]),
            qg=np.ascontiguousarray(np.tile(na_q_g[l], 16)[None]), kg=np.ascontiguousarray(np.tile(na_k_g[l], 16)[None]),
            w_o=np.ascontiguousarray(w_out[l]), b_o=np.ascontiguousarray(b_out[l][None]),
        )
        for core in range(8):
            b, s = divmod(core, NS)
            lo, hi = s * TL, (s + 1) * TL
            xo = np.concatenate([ctx[b], x[b, :lo], x[b, hi:]], 0)
            xh = np.zeros((512, D), np.float32)
            if lo >= 256:
                xh[0:256] = x[b, lo - 256:lo]
            if hi + 256 <= T:
                xh[256:512] = x[b, hi:hi + 256]
            cv = np.stack([np.asarray(c[b], np.float32), np.asarray(c_ctx, np.float32)], 0)
            cvec = np.ascontiguousarray(cv.reshape(2, 16, 128).transpose(2, 0, 1).reshape(128, 32))
            pos_f = np.concatenate([np.zeros(CTX, np.int64), np.arange(lo, hi)])
            is_ctx = np.zeros(len(pos_f), bool); is_ctx[:CTX] = True
            pos_o = np.concatenate([np.arange(0, lo), np.arange(hi, T)])
            gm = np.zeros((4, max(NOL, 1) * 128), np.float32)
            nb = lo
            gm[0, nb:] = NEG; gm[1, nb:] = -NEG
            gm[2, :nb] = NEG; gm[3, :nb] = -NEG
            m = dict(wl)
            m.update(xl=np.ascontiguousarray(x[b, lo:hi]), xo=np.ascontiguousarray(xo), xh=xh, cvec=cvec,
                     nat=_na_tables(np.asarray(na_rpb[l], np.float32), s, NT, rows),
                     rope_f=_rope_rows(pos_f, is_ctx), rope_o=_rope_rows(pos_o, None) if NOL > 0 else np.zeros((128, 128), np.float32),
                     gmask=gm)
            in_maps.append(m)
        if _DEBUG_HOOK is not None:
            return _DEBUG_HOOK(in_maps)
        res = run_bass_kernel_spmd(nc, in_maps, core_ids=list(range(8)))
        xn = np.empty_like(x); cn = np.empty_like(ctx)
        for core in range(8):
            b, s = divmod(core, NS)
            xn[b, s * TL:(s + 1) * TL] = res.results[core]["out_l"]
            if s == 0:
                cn[b] = res.results[core]["out_c"]
        x, ctx = xn, cn
    return x
```
